# Optimizing a Trainium2 kernel written in Bass

```python
import jax
import jax.numpy as jnp
from jax import lax
import numpy as np

D_MODEL = 1024
BATCH = 1
SEQ = 16384
DEPTH = 4

GRID_W = 64
N_MEM = 256
N_EVEN = (DEPTH + 1) // 2
N_ODD = DEPTH // 2
MIX_W = D_MODEL // 2
A_HEADS = 4
A_DK = MIX_W // A_HEADS
A_DV = A_DK
CHUNK = 64
POOL_WINDOWS = (2, 4, 8, 16)
POOL_GROUPS = len(POOL_WINDOWS)
POOL_GW = MIX_W // POOL_GROUPS
C_HEADS = 4
C_KV_HEADS = 2
C_HD = MIX_W // C_HEADS
KV_W = C_KV_HEADS * C_HD
ROPE_THETA = 10000.0
Q_BLOCK = 128
CONV_W = 31
GLU_W = 2 * MIX_W
XA_HEADS = 4
XA_HD = D_MODEL // XA_HEADS
D_FF = -(-8 * D_MODEL // (3 * 256)) * 256
AB_IN = 6 * MIX_W
CD_IN = MIX_W + 2 * KV_W + GLU_W
ALPHA = (2 * DEPTH) ** 0.25
BETA = (8 * DEPTH) ** -0.25
EPS = 1e-6
F32 = jnp.float32

kernel_name = 'hybrid_hgrn2_pool_gqa_conformer_encoder'


def layer_norm(x, g, b):
    xf = x.astype(F32)
    mu = jnp.mean(xf, axis=-1, keepdims=True)
    var = jnp.mean(jnp.square(xf - mu), axis=-1, keepdims=True)
    return ((xf - mu) * lax.rsqrt(var + EPS) * g + b).astype(x.dtype)


def rms_norm(x, g):
    xf = x.astype(F32)
    return (xf * lax.rsqrt(jnp.mean(xf * xf, axis=-1, keepdims=True) + EPS) * g).astype(x.dtype)


def hgrn2_scan(q, k, v, logf):
    Bn, S, H, K = q.shape
    V = v.shape[-1]
    nc = S // CHUNK

    def chunks(a):
        return a.reshape(Bn, nc, CHUNK, H, a.shape[-1]).transpose(1, 0, 3, 2, 4)

    lower = jnp.tril(jnp.ones((CHUNK, CHUNK), bool))[:, :, None]

    def step(state, inp):
        qc, kc, vc, gc = inp
        b = jnp.cumsum(gc, axis=2)
        o_inter = jnp.einsum('bhck,bhkv->bhcv', qc * jnp.exp(b), state)
        rel = jnp.where(lower, b[:, :, :, None, :] - b[:, :, None, :, :], -jnp.inf)
        scores = jnp.einsum('bhtk,bhsk,bhtsk->bhts', qc, kc, jnp.exp(rel))
        o_intra = jnp.einsum('bhts,bhsv->bhtv', scores, vc)
        b_last = b[:, :, -1, :]
        k_dec = kc * jnp.exp(b_last[:, :, None, :] - b)
        state = jnp.exp(b_last)[..., None] * state + jnp.einsum('bhck,bhcv->bhkv', k_dec, vc)
        return state, o_inter + o_intra

    state0 = jnp.zeros((Bn, H, K, V), F32)
    _, out = lax.scan(step, state0, (chunks(q), chunks(k), chunks(v), chunks(logf)))
    return out.transpose(1, 0, 3, 2, 4).reshape(Bn, S, H, V)


def hgrn2_gates(z, lb):
    lb = lb.reshape(A_HEADS, A_DK)
    logf = jnp.logaddexp(jnp.log(lb), jnp.log1p(-lb) + jax.nn.log_sigmoid(z))
    k = (1.0 - lb) * jax.nn.sigmoid(-z)
    return k, logf


def multiscale_pool(u, pool_w, pool_scale):
    Bn, S, _ = u.shape
    uf = u.astype(F32)
    P = jnp.concatenate([jnp.zeros((Bn, 1, MIX_W), F32), jnp.cumsum(uf, axis=1)], axis=1)
    t = jnp.arange(S)
    outs = []
    for gi, w in enumerate(POOL_WINDOWS):
        lo = jnp.clip(t - w // 2, 0, S - 1)
        hi = jnp.clip(t - w // 2 + w - 1, 0, S - 1)
        sl = slice(gi * POOL_GW, (gi + 1) * POOL_GW)
        Pg = P[:, :, sl]
        cnt = (hi - lo + 1).astype(F32)[None, :, None]
        outs.append((Pg[:, hi + 1] - Pg[:, lo]) / cnt - uf[:, :, sl])
    d = jnp.stack(outs, axis=2)
    y = jnp.einsum('bsgc,gcd->bsgd', d, pool_w.astype(F32)).reshape(Bn, S, MIX_W)
    return (y * pool_scale).astype(u.dtype)


def even_mixer(x, w_in, lb_fwd, lb_bwd, norm_g, pool_w, pool_scale, w_out):
    Bn, S, _ = x.shape
    h = x @ w_in
    q, i, zf, zb, og, u = jnp.split(h, 6, axis=-1)

    def heads(a):
        return a.reshape(Bn, S, A_HEADS, A_DK).astype(F32)

    qh = heads(jax.nn.silu(q))
    vh = heads(i)
    kf, gf = hgrn2_gates(heads(zf), lb_fwd)
    kb, gb = hgrn2_gates(heads(zb), lb_bwd)
    flip = lambda a: jnp.flip(a, axis=1)
    o = hgrn2_scan(qh, kf, vh, gf) + flip(hgrn2_scan(flip(qh), flip(kb), flip(vh), flip(gb)))
    y_a = rms_norm(o, norm_g.reshape(A_HEADS, A_DV)).reshape(Bn, S, MIX_W).astype(x.dtype) * jax.nn.silu(og)
    y_b = multiscale_pool(u, pool_w, pool_scale)
    return jnp.concatenate([y_a, y_b], axis=-1) @ w_out


def axial_rope_tables(S):
    rows = S // GRID_W
    row = jnp.repeat(jnp.arange(rows), GRID_W)
    col = jnp.tile(jnp.arange(GRID_W), rows)
    half = C_HD // 2
    freqs = ROPE_THETA ** (-jnp.arange(0, half, 2, dtype=F32) / half)

    def ang(p):
        a = p.astype(F32)[:, None] * freqs[None, :]
        return jnp.concatenate([a, a], axis=-1)

    angles = jnp.concatenate([ang(row), ang(col)], axis=-1)
    return jnp.cos(angles), jnp.sin(angles)


def apply_rope(x, cos, sin):
    xs = x.reshape(x.shape[:-1] + (2, 2, C_HD // 4))
    rot = jnp.stack([-xs[..., 1, :], xs[..., 0, :]], axis=-2).reshape(x.shape)
    return (x * cos[None, :, None, :] + rot * sin[None, :, None, :]).astype(x.dtype)


def block_attention(q, k, v):
    Bn, S = q.shape[:2]
    G = C_HEADS // C_KV_HEADS
    nb = S // Q_BLOCK
    qb = q.reshape(Bn, nb, Q_BLOCK, C_KV_HEADS, G, C_HD).transpose(1, 0, 2, 3, 4, 5)
    scale = C_HD ** -0.5

    def one_block(qi):
        s = jnp.einsum('bqhgd,bkhd->bhgqk', qi, k).astype(F32) * scale
        p = jax.nn.softmax(s, axis=-1).astype(v.dtype)
        return jnp.einsum('bhgqk,bkhd->bqhgd', p, v)

    out = lax.map(one_block, qb)
    return out.transpose(1, 0, 2, 3, 4, 5).reshape(Bn, S, C_HEADS * C_HD)


def conformer_conv(a, conv_w, conv_b, ln_g, ln_b):
    val, gate = jnp.split(a, 2, axis=-1)
    u = val * jax.nn.sigmoid(gate)
    u = lax.conv_general_dilated(
        u, conv_w[:, None, :], window_strides=(1,),
        padding=[(CONV_W // 2, CONV_W // 2)],
        dimension_numbers=('NWC', 'WIO', 'NWC'),
        feature_group_count=MIX_W) + conv_b
    return jax.nn.silu(layer_norm(u, ln_g, ln_b))


def odd_mixer(x, w_in, q_g, k_g, conv_w, conv_b, ln_g, ln_b, w_out):
    Bn, S, _ = x.shape
    h = x @ w_in
    q, k, v, a = jnp.split(h, [MIX_W, MIX_W + KV_W, MIX_W + 2 * KV_W], axis=-1)
    q = rms_norm(q.reshape(Bn, S, C_HEADS, C_HD), q_g)
    k = rms_norm(k.reshape(Bn, S, C_KV_HEADS, C_HD), k_g)
    v = v.reshape(Bn, S, C_KV_HEADS, C_HD)
    cos, sin = axial_rope_tables(S)
    y_c = block_attention(apply_rope(q, cos, sin), apply_rope(k, cos, sin), v)
    y_d = conformer_conv(a, conv_w, conv_b, ln_g, ln_b)
    return jnp.concatenate([y_c, y_d], axis=-1) @ w_out


def memory_cross_attention(x, mem, wq, wkv, wo):
    Bn, S, _ = x.shape
    M = mem.shape[1]
    q = (x @ wq).reshape(Bn, S, XA_HEADS, XA_HD)
    k, v = jnp.split(mem @ wkv, 2, axis=-1)
    k = k.reshape(Bn, M, XA_HEADS, XA_HD)
    v = v.reshape(Bn, M, XA_HEADS, XA_HD)
    s = jnp.einsum('bshd,bmhd->bhsm', q, k).astype(F32) * (XA_HD ** -0.5)
    p = jax.nn.softmax(s, axis=-1).astype(x.dtype)
    o = jnp.einsum('bhsm,bmhd->bshd', p, v).reshape(Bn, S, D_MODEL)
    return o @ wo


def swiglu_ffn(x, w_gu, w_down):
    gt, up = jnp.split(x @ w_gu, 2, axis=-1)
    return (jax.nn.silu(gt) * up) @ w_down


def setup_inputs(seed: int = 0) -> dict:
    key = jax.random.key(seed)
    ks = jax.random.split(key, 24)

    def nrm(k, shape, scale):
        return jax.random.normal(k, shape, F32) * scale

    def gain(k, shape):
        return 1.0 + nrm(k, shape, 0.02)

    x = nrm(ks[0], (BATCH, SEQ, D_MODEL), 1.0)
    mem = nrm(ks[1], (BATCH, N_MEM, D_MODEL), 1.0)
    w_in_ab = nrm(ks[2], (N_EVEN, D_MODEL, AB_IN), D_MODEL ** -0.5)
    hgrn_lb_logits = nrm(ks[3], (2, DEPTH, MIX_W), 0.5)
    hgrn_norm_g = gain(ks[4], (N_EVEN, MIX_W))
    pool_w = nrm(ks[5], (N_EVEN, POOL_GROUPS, POOL_GW, POOL_GW), POOL_GW ** -0.5)
    pool_scale = gain(ks[6], (N_EVEN, MIX_W))
    w_out_ab = nrm(ks[7], (N_EVEN, 2 * MIX_W, D_MODEL), BETA * (2 * MIX_W) ** -0.5)
    w_in_cd = nrm(ks[8], (N_ODD, D_MODEL, CD_IN), D_MODEL ** -0.5)
    q_norm_g = gain(ks[9], (N_ODD, C_HD))
    k_norm_g = gain(ks[10], (N_ODD, C_HD))
    conv_w = nrm(ks[11], (N_ODD, CONV_W, MIX_W), CONV_W ** -0.5)
    conv_b = nrm(ks[12], (N_ODD, MIX_W), 0.02)
    conv_ln_g = gain(ks[13], (N_ODD, MIX_W))
    conv_ln_b = nrm(ks[14], (N_ODD, MIX_W), 0.02)
    w_out_cd = nrm(ks[15], (N_ODD, 2 * MIX_W, D_MODEL), BETA * (2 * MIX_W) ** -0.5)
    xa_wq = nrm(ks[16], (DEPTH, D_MODEL, D_MODEL), D_MODEL ** -0.5)
    xa_wkv = jnp.concatenate([
        nrm(ks[17], (DEPTH, D_MODEL, D_MODEL), D_MODEL ** -0.5),
        nrm(ks[18], (DEPTH, D_MODEL, D_MODEL), BETA * D_MODEL ** -0.5)], axis=-1)
    xa_wo = nrm(ks[19], (DEPTH, D_MODEL, D_MODEL), BETA * D_MODEL ** -0.5)
    ffn_w_gu = nrm(ks[20], (DEPTH, D_MODEL, 2 * D_FF), BETA * D_MODEL ** -0.5)
    ffn_w_down = nrm(ks[21], (DEPTH, D_FF, D_MODEL), BETA * D_FF ** -0.5)
    ln_g = gain(ks[22], (DEPTH, 3, D_MODEL))
    ln_b = nrm(ks[23], (DEPTH, 3, D_MODEL), 0.02)
    return {'x': x, 'mem': mem, 'w_in_ab': w_in_ab, 'hgrn_lb_logits': hgrn_lb_logits,
            'hgrn_norm_g': hgrn_norm_g, 'pool_w': pool_w, 'pool_scale': pool_scale,
            'w_out_ab': w_out_ab, 'w_in_cd': w_in_cd, 'q_norm_g': q_norm_g,
            'k_norm_g': k_norm_g, 'conv_w': conv_w, 'conv_b': conv_b,
            'conv_ln_g': conv_ln_g, 'conv_ln_b': conv_ln_b, 'w_out_cd': w_out_cd,
            'xa_wq': xa_wq, 'xa_wkv': xa_wkv, 'xa_wo': xa_wo, 'ffn_w_gu': ffn_w_gu,
            'ffn_w_down': ffn_w_down, 'ln_g': ln_g, 'ln_b': ln_b}


def reference(x, mem, w_in_ab, hgrn_lb_logits, hgrn_norm_g, pool_w, pool_scale, w_out_ab,
              w_in_cd, q_norm_g, k_norm_g, conv_w, conv_b, conv_ln_g, conv_ln_b, w_out_cd,
              xa_wq, xa_wkv, xa_wo, ffn_w_gu, ffn_w_down, ln_g, ln_b):
    cum = jnp.cumsum(jax.nn.softmax(hgrn_lb_logits.astype(F32), axis=1), axis=1)
    lb = jnp.maximum(cum - cum[:, :1], 0.0)
    for l in range(DEPTH):
        j = l // 2
        if l % 2 == 0:
            y = even_mixer(x, w_in_ab[j], lb[0, l], lb[1, l], hgrn_norm_g[j],
                           pool_w[j], pool_scale[j], w_out_ab[j])
        else:
            y = odd_mixer(x, w_in_cd[j], q_norm_g[j], k_norm_g[j], conv_w[j], conv_b[j],
                          conv_ln_g[j], conv_ln_b[j], w_out_cd[j])
        x = layer_norm(ALPHA * x + y, ln_g[l, 0], ln_b[l, 0])
        x = layer_norm(ALPHA * x + memory_cross_attention(x, mem, xa_wq[l], xa_wkv[l], xa_wo[l]),
                       ln_g[l, 1], ln_b[l, 1])
        x = layer_norm(ALPHA * x + swiglu_ffn(x, ffn_w_gu[l], ffn_w_down[l]),
                       ln_g[l, 2], ln_b[l, 2])
    return x
```

```python
import numpy as np
from contextlib import ExitStack
import concourse.bass as bass
import concourse.mybir as mybir
from concourse.bass_utils import run_bass_kernel_spmd

F32 = mybir.dt.float32
BF16 = mybir.dt.bfloat16
AF = mybir.ActivationFunctionType
ALU = mybir.AluOpType
AX = mybir.AxisListType

D = 1024
KC = 8
DFF = 2816
JC = 22
NMEM = 256
ALPHA = 8.0 ** 0.25
EPS = 1e-6
HALO = 16


class Eng:
    def __init__(self, name, e, sem):
        self.name, self.e, self.sem = name, e, sem
        self.count = 0
        self.seen = {}
        self.seen_dma = {}


class Tl:
    def __init__(self, name, t):
        self.name, self.t = name, t
        self.w = None
        self.r = []
        self.ds = None

    def __getitem__(self, idx):
        return self.t[idx]


class FW:
    def __init__(self, nc, stack):
        self.nc, self.stack = nc, stack
        self.engs = {}
        for name, e in (("pe", nc.tensor), ("dve", nc.vector), ("act", nc.scalar),
                        ("pool", nc.gpsimd), ("sp", nc.sync)):
            sem = stack.enter_context(nc.semaphore("s_" + name))
            self.engs[name] = Eng(name, e, sem)
        self.all_dsems = []
        self.uid = 0

    def sb(self, name, shape, dt, stack=None):
        self.uid += 1
        t = (stack or self.stack).enter_context(
            self.nc.sbuf_tensor(f"{name}_{self.uid}", list(shape), dt))
        return Tl(name, t)

    def ps(self, name, shape, dt):
        t = self.stack.enter_context(self.nc.psum_tensor(name, list(shape), dt))
        return Tl(name, t)

    def dsem(self, name):
        s = self.stack.enter_context(self.nc.semaphore("d_" + name))
        d = {"sem": s, "val": 0, "name": name}
        self.all_dsems.append(d)
        return d

    def _need(self, eng, dep, we, wd, relax):
        if dep is None:
            return
        if dep[0] == 'e':
            _, oe, n = dep
            if oe is eng and relax and eng.name == "pe":
                return
            if eng.seen.get(oe.name, 0) >= n:
                return
            we[oe.name] = max(we.get(oe.name, 0), n)
        else:
            _, d, val = dep
            key = id(d)
            if eng.seen_dma.get(key, 0) >= val:
                return
            cur = wd.get(key)
            if cur is None or cur[1] < val:
                wd[key] = (d, val)

    def _sync(self, eng, reads, writes):
        we, wd = {}, {}
        for t in reads:
            self._need(eng, t.w, we, wd, False)
        for t in writes:
            self._need(eng, t.w, we, wd, True)
            for r in t.r:
                self._need(eng, r, we, wd, True)
        for oname, n in we.items():
            eng.e.wait_ge(self.engs[oname].sem, n)
            eng.seen[oname] = n
        for key, (d, val) in wd.items():
            eng.e.wait_ge(d["sem"], val)
            eng.seen_dma[key] = val

    def _mark(self, dep, reads, writes):
        for t in reads:
            t.r.append(dep)
            if len(t.r) > 16:
                best = {}
                for d in t.r:
                    k = (d[0], id(d[1]))
                    if k not in best or best[k][2] < d[2]:
                        best[k] = d
                t.r = list(best.values())
        for t in writes:
            t.w = dep
            t.r = []

    def op(self, ename, fn, reads=(), writes=()):
        eng = self.engs[ename]
        self._sync(eng, reads, writes)
        inst = fn(eng.e)
        eng.count += 1
        inst.then_inc(eng.sem, 1)
        self._mark(('e', eng, eng.count), reads, writes)
        return inst

    def dma(self, ename, out_ap, in_ap, dsem=None, reads=(), writes=(), **kw):
        if dsem is None:
            tl = (list(writes) + list(reads))[0]
            if getattr(tl, "ds", None) is None:
                tl.ds = self.dsem(f"t{self.uid}_{len(self.all_dsems)}")
            dsem = tl.ds
        eng = self.engs[ename]
        self._sync(eng, reads, writes)
        inst = eng.e.dma_start(out=out_ap, in_=in_ap, **kw)
        dsem["val"] += 16
        inst.then_inc(dsem["sem"], 16)
        self._mark(('d', dsem, dsem["val"]), reads, writes)
        return inst

    def barrier(self):
        for eng in self.engs.values():
            for o in self.engs.values():
                if o is eng or o.count == 0:
                    continue
                if eng.seen.get(o.name, 0) < o.count:
                    eng.e.wait_ge(o.sem, o.count)
                    eng.seen[o.name] = o.count
            for d in self.all_dsems:
                if d["val"] > 0 and eng.seen_dma.get(id(d), 0) < d["val"]:
                    eng.e.wait_ge(d["sem"], d["val"])
                    eng.seen_dma[id(d)] = d["val"]

    def finish(self):
        eng = self.engs["sp"]
        for d in self.all_dsems:
            if d["val"] > 0:
                eng.e.wait_ge(d["sem"], d["val"])


class Ctx:
    pass


def mm_group(fw, out_tl, items, reads):
    def fn(e):
        inst = None
        for (o, l, r, s0, s1) in items:
            inst = e.matmul(o, lhsT=l, rhs=r, start=s0, stop=s1)
        return inst
    return fw.op("pe", fn, reads=reads, writes=[out_tl])


class WPool:
    def __init__(self, fw, n, elems, stack=None):
        self.slots = [(fw.sb(f"wslot{i}", [128, elems], BF16, stack), None) for i in range(n)]
        self.i = 0

    def next(self):
        s = self.slots[self.i % len(self.slots)]
        self.i += 1
        return s


def load_w_kcn(fw, wp, w_dram, col0, ncols, nk=KC):
    tl, ds = wp.next()
    view = tl.t[:, 0:nk * ncols].rearrange("p (k n) -> p k n", k=nk)
    src = w_dram.rearrange("(k p) n -> p k n", p=128)[:, :, col0:col0 + ncols]
    fw.dma("sp", view, src, writes=[tl])
    return tl, view


def pipeline(n, load, compute, pf=2):
    loaded = {}
    for i in range(min(pf, n)):
        loaded[i] = load(i)
    for i in range(n):
        if i + pf < n:
            loaded[i + pf] = load(i + pf)
        compute(i, loaded.pop(i))


def make_consts(fw, c, consts_dram):
    c.cst_f = fw.sb("cst_f", [128, 256], F32)
    fw.dma("sp", c.cst_f[:], consts_dram[:, 0:256], writes=[c.cst_f])
    c.ident = fw.sb("ident", [128, 128], BF16)
    c.ones_bf = fw.sb("ones_bf", [128, 128], BF16)
    c.ones_f = fw.sb("ones_f", [128, 128], F32)
    fw.op("dve", lambda e: e.tensor_copy(out=c.ident[:], in_=c.cst_f[:, 0:128]), reads=[c.cst_f], writes=[c.ident])
    fw.op("dve", lambda e: e.tensor_copy(out=c.ones_bf[:], in_=c.cst_f[:, 128:256]), reads=[c.cst_f], writes=[c.ones_bf])
    fw.op("dve", lambda e: e.tensor_copy(out=c.ones_f[:], in_=c.cst_f[:, 128:256]), reads=[c.cst_f], writes=[c.ones_f])
    c.eps = fw.sb("eps", [128, 1], F32)
    fw.op("pool", lambda e: e.memset(c.eps[:], EPS), writes=[c.eps])


def host_consts():
    a = np.zeros((128, 256), np.float32)
    a[:, 0:128] = np.eye(128, dtype=np.float32)
    a[:, 128:256] = 1.0
    return a


def make_psum(fw, c):
    c.pY = [fw.ps(f"pY{i}", [128, 1024], F32) for i in range(2)]
    c.pT = fw.ps("pT", [128, 1024], BF16)
    c.pS = [fw.ps(f"pS{i}", [128, 512], F32) for i in range(3)]
    c.pS_i = 0
    c.pY_i = 0


def next_pS(c):
    p = c.pS[c.pS_i % 3]
    c.pS_i += 1
    return p


def next_pY(c):
    p = c.pY[c.pY_i % 2]
    c.pY_i += 1
    return p


def transpose_to_xT(fw, c, src_bf, dst_tl, dst_view):
    def fn(e):
        inst = None
        for k in range(KC):
            inst = e.transpose(c.pT[:, k * 128:(k + 1) * 128], src_bf[:, k * 128:(k + 1) * 128], c.ident[:])
        return inst
    fw.op("pe", fn, reads=[src_bf, c.ident], writes=[c.pT])
    fw.op("act", lambda e: e.copy(out=dst_view, in_=c.pT[:].rearrange("p (k n) -> p k n", k=KC)),
          reads=[c.pT], writes=[dst_tl])


def ln_epilogue(fw, c, xres_t, y_ps, g_bc, b_bc, out_eng="pool"):
    t = c.ln_tmp[c.ln_i % 2]
    st = c.ln_st[c.ln_i % 2]
    c.ln_i += 1
    fw.op("dve", lambda e: e.scalar_tensor_tensor(out=t[:], in0=xres_t[:], scalar=ALPHA, in1=y_ps[:],
                                                  op0=ALU.mult, op1=ALU.add),
          reads=[xres_t, y_ps], writes=[t])
    fw.op("dve", lambda e: e.bn_stats(out=st[:, 0:6], in_=t[:, 0:512]), reads=[t], writes=[st])
    fw.op("dve", lambda e: e.bn_stats(out=st[:, 6:12], in_=t[:, 512:1024]), reads=[t], writes=[st])
    fw.op("dve", lambda e: e.bn_aggr(out=st[:, 12:14], in_=st[:, 0:12]), reads=[st], writes=[st])
    fw.op("act", lambda e: e.activation(out=st[:, 14:15], in_=st[:, 13:14], func=AF.Sqrt, bias=c.eps[:], scale=1.0),
          reads=[st, c.eps], writes=[st])
    fw.op("dve", lambda e: e.reciprocal(out=st[:, 14:15], in_=st[:, 14:15]), reads=[st], writes=[st])
    fw.op("dve", lambda e: e.scalar_tensor_tensor(out=st[:, 15:16], in0=st[:, 12:13], scalar=-1.0, in1=st[:, 14:15],
                                                  op0=ALU.mult, op1=ALU.mult), reads=[st], writes=[st])
    fw.op("act", lambda e: e.activation(out=t[:], in_=t[:], func=AF.Identity, bias=st[:, 15:16], scale=st[:, 14:15]),
          reads=[t, st], writes=[t])
    fw.op(out_eng, lambda e: e.tensor_tensor(out=t[:], in0=t[:], in1=g_bc[:], op=ALU.mult), reads=[t, g_bc], writes=[t])
    fw.op("dve", lambda e: e.tensor_tensor(out=xres_t[:], in0=t[:], in1=b_bc[:], op=ALU.add),
          reads=[t, b_bc], writes=[xres_t])


def make_xT_tile(fw, c, xres, nt0, ntn, dst_tl, dst3):
    for i in range(ntn):
        xb = c.xb[c.xb_i % 2]
        c.xb_i += 1
        src = xres[nt0 + i]
        fw.op("pool", lambda e: e.tensor_copy(out=xb[:], in_=src[:]), reads=[src], writes=[xb])
        transpose_to_xT(fw, c, xb, dst_tl, dst3[:, :, i * 128:(i + 1) * 128])


def load_bc(fw, dst, vec_dram):
    fw.dma("sp", dst[:], vec_dram.partition_broadcast(128), writes=[dst])


def build_tail(TPC):
    NT = TPC // 128
    NQ = TPC // 512
    nc = bass.Bass("TRN2", target_bir_lowering=False)
    x_in = nc.dram_tensor("x", [TPC, D], F32, kind="ExternalInput").ap()
    mem = nc.dram_tensor("mem", [NMEM, D], F32, kind="ExternalInput").ap()
    wq = nc.dram_tensor("wq", [D, D], BF16, kind="ExternalInput").ap()
    wkv = nc.dram_tensor("wkv", [D, 2 * D], BF16, kind="ExternalInput").ap()
    wo = nc.dram_tensor("wo", [D, D], BF16, kind="ExternalInput").ap()
    wgu = nc.dram_tensor("wgu", [D, 2 * DFF], BF16, kind="ExternalInput").ap()
    wdn = nc.dram_tensor("wdn", [DFF, D], BF16, kind="ExternalInput").ap()
    lnp = nc.dram_tensor("lnp", [4, D], F32, kind="ExternalInput").ap()
    consts = nc.dram_tensor("consts", [128, 256], F32, kind="ExternalInput").ap()
    x_out = nc.dram_tensor("y", [TPC, D], F32, kind="ExternalOutput").ap()

    with ExitStack() as st:
        fw = FW(nc, st)
        c = Ctx()
        make_consts(fw, c, consts)
        make_psum(fw, c)
        xres = [fw.sb(f"xres{i}", [128, D], F32) for i in range(NT)]
        for i in range(NT):
            fw.dma("sp", xres[i][:], x_in[i * 128:(i + 1) * 128, :], writes=[xres[i]])
        c.ln_tmp = [fw.sb(f"lnt{i}", [128, D], F32) for i in range(2)]
        c.ln_st = [fw.sb(f"lnst{i}", [128, 16], F32) for i in range(2)]
        c.ln_i = 0
        c.xb = [fw.sb(f"xb{i}", [128, D], BF16) for i in range(2)]
        c.xb_i = 0
        lnb = [fw.sb(f"lnb{i}", [128, D], F32) for i in range(2)]
        for i in range(2):
            load_bc(fw, lnb[i], lnp[i, :])
        xT = [fw.sb(f"xT{i}", [128, KC * 512], BF16) for i in range(2)]
        xT3 = [t.t[:].rearrange("p (k n) -> p k n", k=KC) for t in xT]
        wp = WPool(fw, 3, KC * 512)

        with ExitStack() as ph:
            memT = fw.sb("memT", [128, KC * NMEM], BF16, ph)
            memT3 = memT.t[:].rearrange("p (k n) -> p k n", k=KC)
            kmT = fw.sb("kmT", [128, KC * NMEM], BF16, ph)
            kmT3 = kmT.t[:].rearrange("p (k n) -> p k n", k=KC)
            vm = fw.sb("vm", [128, 2 * D], BF16, ph)
            vm3 = vm.t[:].rearrange("p (t n) -> p t n", t=2)
            wo_sb = fw.sb("wo_sb", [128, KC * D], BF16, ph)
            wo3 = wo_sb.t[:].rearrange("p (k n) -> p k n", k=KC)
            qT = fw.sb("qT", [128, KC * 512], BF16, ph)
            qT3 = qT.t[:].rearrange("p (k n) -> p k n", k=KC)
            oT = fw.sb("oT", [128, KC * 512], BF16, ph)
            oT3 = oT.t[:].rearrange("p (k n) -> p k n", k=KC)
            pT_sb = [fw.sb(f"pTs{i}", [128, 512], BF16, ph) for i in range(4)]
            rs = fw.sb("rs", [128, 512], F32, ph)
            mem_f = fw.sb("mem_f", [128, D], F32, ph)
            fw.dma("sp", wo3, wo.rearrange("(k p) n -> p k n", p=128), writes=[wo_sb])
            for mt in range(2):
                fw.dma("sp", mem_f[:], mem[mt * 128:(mt + 1) * 128, :], writes=[mem_f])
                xb = c.xb[c.xb_i % 2]; c.xb_i += 1
                fw.op("dve", lambda e: e.tensor_copy(out=xb[:], in_=mem_f[:]), reads=[mem_f], writes=[xb])
                transpose_to_xT(fw, c, xb, memT, memT3[:, :, mt * 128:(mt + 1) * 128])
            for piece in range(2):
                wtl, wv = load_w_kcn(fw, wp, wkv, piece * 512, 512)
                for j in range(4):
                    p = next_pS(c)
                    mm_group(fw, p, [(p[:, 0:NMEM], wv[:, k, j * 128:(j + 1) * 128], memT3[:, k, :], k == 0, k == KC - 1)
                                     for k in range(KC)], reads=[wtl, memT])
                    fw.op("act", lambda e: e.copy(out=kmT3[:, piece * 4 + j, :], in_=p[:, 0:NMEM]), reads=[p], writes=[kmT])
            for piece in range(2):
                wtl, wv = load_w_kcn(fw, wp, wkv, D + piece * 512, 512)
                for mt in range(2):
                    p = next_pS(c)
                    mm_group(fw, p, [(p[:, :], memT3[:, k, mt * 128:(mt + 1) * 128], wv[:, k, :], k == 0, k == KC - 1)
                                     for k in range(KC)], reads=[wtl, memT])
                    fw.op("act", lambda e: e.copy(out=vm3[:, mt, piece * 512:(piece + 1) * 512], in_=p[:, :]),
                          reads=[p], writes=[vm])

            for q in range(NQ):
                xt_tl, xt3 = xT[q % 2], xT3[q % 2]
                make_xT_tile(fw, c, xres, q * 4, 4, xt_tl, xt3)
                def ld(i):
                    return load_w_kcn(fw, wp, wq, i * 512, 512)

                def cp(i, lw):
                    wtl, wv = lw
                    for j in range(4):
                        p = next_pS(c)
                        mm_group(fw, p, [(p[:, :], wv[:, k, j * 128:(j + 1) * 128], xt3[:, k, :], k == 0, k == KC - 1)
                                         for k in range(KC)], reads=[wtl, xt_tl])
                        fw.op("act", lambda e: e.copy(out=qT3[:, i * 4 + j, :], in_=p[:, :]), reads=[p], writes=[qT])
                pipeline(2, ld, cp, pf=2)
                for h in range(4):
                    pts = []
                    for mt in range(2):
                        p = next_pS(c)
                        mm_group(fw, p, [(p[:, :], kmT3[:, 2 * h + dc, mt * 128:(mt + 1) * 128], qT3[:, 2 * h + dc, :],
                                          dc == 0, dc == 1) for dc in range(2)], reads=[kmT, qT])
                        pt = pT_sb[(h * 2 + mt) % 4]
                        fw.op("act", lambda e: e.activation(out=pt[:], in_=p[:, :], func=AF.Exp, scale=1.0 / 16.0),
                              reads=[p], writes=[pt])
                        pts.append(pt)
                    psum_s = next_pS(c)
                    mm_group(fw, psum_s, [(psum_s[:, :], c.ones_bf[:], pts[mt][:], mt == 0, mt == 1) for mt in range(2)],
                             reads=[c.ones_bf] + pts)
                    fw.op("dve", lambda e: e.reciprocal(out=rs[:], in_=psum_s[:, :]), reads=[psum_s], writes=[rs])
                    for dc in range(2):
                        p = next_pS(c)
                        col = h * 256 + dc * 128
                        mm_group(fw, p, [(p[:, :], vm3[:, mt, col:col + 128], pts[mt][:], mt == 0, mt == 1) for mt in range(2)],
                                 reads=[vm] + pts)
                        fw.op("dve", lambda e: e.tensor_tensor(out=oT3[:, 2 * h + dc, :], in0=p[:, :], in1=rs[:], op=ALU.mult),
                              reads=[p, rs], writes=[oT])
                for i in range(4):
                    nt = q * 4 + i
                    py = next_pY(c)
                    mm_group(fw, py, [(py[:, hf * 512:(hf + 1) * 512], oT3[:, k, i * 128:(i + 1) * 128],
                                       wo3[:, k, hf * 512:(hf + 1) * 512], k == 0, k == KC - 1)
                                      for hf in range(2) for k in range(KC)], reads=[oT, wo_sb])
                    ln_epilogue(fw, c, xres[nt], py, lnb[0], lnb[1])
            fw.barrier()

        with ExitStack() as ph:
            wdn_sb = fw.sb("wdn_sb", [128, JC * D], BF16, ph)
            wdn3 = wdn_sb.t[:].rearrange("p (j n) -> p j n", j=JC)
            actT = fw.sb("actT", [128, JC * 512], BF16, ph)
            actT3 = actT.t[:].rearrange("p (j n) -> p j n", j=JC)
            sg = [fw.sb(f"sg{i}", [128, 512], F32, ph) for i in range(2)]
            for i in range(2):
                load_bc(fw, lnb[i], lnp[2 + i, :])
            for j0 in range(0, JC, 6):
                j1 = min(JC, j0 + 6)
                fw.dma("sp", wdn3[:, j0:j1, :], wdn.rearrange("(j p) n -> p j n", p=128)[:, j0:j1, :], writes=[wdn_sb])
            pieces = []
            for j0 in range(0, JC, 4):
                nj = min(4, JC - j0)
                pieces.append((j0, nj))
            for q in range(NQ):
                xt_tl, xt3 = xT[q % 2], xT3[q % 2]
                make_xT_tile(fw, c, xres, q * 4, 4, xt_tl, xt3)

                def ld(i):
                    j0, nj = pieces[i]
                    g = load_w_kcn(fw, wp, wgu, j0 * 128, nj * 128)
                    return g

                def ld2(i):
                    j0, nj = pieces[i]
                    return load_w_kcn(fw, wp, wgu, DFF + j0 * 128, nj * 128)

                def ldx(s):
                    return ld(s // 2) if s % 2 == 0 else ld2(s // 2)

                hold = {}

                def cpx(s, lw):
                    if s % 2 == 0:
                        hold['g'] = lw
                        return
                    gtl, gv = hold['g']
                    utl, uv = lw
                    j0, nj = pieces[s // 2]
                    for j in range(nj):
                        pg = next_pS(c)
                        mm_group(fw, pg, [(pg[:, :], gv[:, k, j * 128:(j + 1) * 128], xt3[:, k, :], k == 0, k == KC - 1)
                                          for k in range(KC)], reads=[gtl, xt_tl])
                        s_t = sg[(j0 + j) % 2]
                        fw.op("act", lambda e: e.activation(out=s_t[:], in_=pg[:, :], func=AF.Silu), reads=[pg], writes=[s_t])
                        pu = next_pS(c)
                        mm_group(fw, pu, [(pu[:, :], uv[:, k, j * 128:(j + 1) * 128], xt3[:, k, :], k == 0, k == KC - 1)
                                          for k in range(KC)], reads=[utl, xt_tl])
                        fw.op("dve", lambda e: e.tensor_tensor(out=actT3[:, j0 + j, :], in0=pu[:, :], in1=s_t[:], op=ALU.mult),
                              reads=[pu, s_t], writes=[actT])
                pipeline(2 * len(pieces), ldx, cpx, pf=1)
                for i in range(4):
                    nt = q * 4 + i
                    py = next_pY(c)
                    mm_group(fw, py, [(py[:, hf * 512:(hf + 1) * 512], actT3[:, j, i * 128:(i + 1) * 128],
                                       wdn3[:, j, hf * 512:(hf + 1) * 512], j == 0, j == JC - 1)
                                      for hf in range(2) for j in range(JC)], reads=[actT, wdn_sb])
                    ln_epilogue(fw, c, xres[nt], py, lnb[0], lnb[1])
                    fw.dma("sp", x_out[nt * 128:(nt + 1) * 128, :], xres[nt][:], reads=[xres[nt]])
            fw.barrier()
        fw.finish()
    return nc


def build_convert(NEL):
    nc = bass.Bass("TRN2", target_bir_lowering=False)
    w = nc.dram_tensor("w", [128, NEL], F32, kind="ExternalInput").ap()
    o = nc.dram_tensor("o", [128, NEL], BF16, kind="ExternalOutput").ap()
    CH = 4096
    with ExitStack() as st:
        fw = FW(nc, st)
        nb = 3
        fin = [fw.sb(f"fin{i}", [128, CH], F32) for i in range(nb)]
        fout = [fw.sb(f"fout{i}", [128, CH], BF16) for i in range(nb)]
        n = (NEL + CH - 1) // CH
        engs = ["dve", "act", "pool"]
        for i in range(n):
            a, b = i * CH, min(NEL, (i + 1) * CH)
            s = i % nb
            fw.dma("sp", fin[s][:, 0:b - a], w[:, a:b], writes=[fin[s]])
            en = engs[i % 3]
            if en == "act":
                fw.op("act", lambda e: e.copy(out=fout[s][:, 0:b - a], in_=fin[s][:, 0:b - a]), reads=[fin[s]], writes=[fout[s]])
            else:
                fw.op(en, lambda e: e.tensor_copy(out=fout[s][:, 0:b - a], in_=fin[s][:, 0:b - a]), reads=[fin[s]], writes=[fout[s]])
            fw.dma("pool", o[:, a:b], fout[s][:, 0:b - a], reads=[fout[s]])
        fw.finish()
    return nc


def host_consts2():
    a = np.zeros((128, 640), np.float32)
    a[:, 0:128] = np.eye(128, dtype=np.float32)
    a[:, 128:256] = 1.0
    P = np.zeros((128, 128), np.float32)
    for half in range(2):
        o = half * 64
        for d in range(32):
            P[o + d, o + d + 32] = -1.0
            P[o + d + 32, o + d] = 1.0
    a[:, 256:384] = P.T
    s = np.arange(128)[:, None]
    t = np.arange(128)[None, :]
    same = (s // 64) == (t // 64)
    a[:, 384:512] = (same & (s <= t)).astype(np.float32)
    a[:, 512:640] = (same & (s >= t)).astype(np.float32)
    return a


def setup_common(fw, c, consts_dram, ncols=640):
    c.cst_f = fw.sb("cst_f", [128, ncols], F32)
    fw.dma("sp", c.cst_f[:], consts_dram[:, 0:ncols], writes=[c.cst_f])
    c.ident = fw.sb("ident", [128, 128], BF16)
    c.ones_bf = fw.sb("ones_bf", [128, 128], BF16)
    fw.op("dve", lambda e: e.tensor_copy(out=c.ident[:], in_=c.cst_f[:, 0:128]), reads=[c.cst_f], writes=[c.ident])
    fw.op("dve", lambda e: e.tensor_copy(out=c.ones_bf[:], in_=c.cst_f[:, 128:256]), reads=[c.cst_f], writes=[c.ones_bf])
    c.eps = fw.sb("eps", [128, 1], F32)
    fw.op("pool", lambda e: e.memset(c.eps[:], EPS), writes=[c.eps])
    make_psum(fw, c)
    c.ln_tmp = [fw.sb(f"lnt{i}", [128, D], F32) for i in range(2)]
    c.ln_st = [fw.sb(f"lnst{i}", [128, 16], F32) for i in range(2)]
    c.ln_i = 0
    c.xb = [fw.sb(f"xb{i}", [128, D], BF16) for i in range(2)]
    c.xb_i = 0


def build_xT_ext(fw, c, x_ext, TPC, xT, xT3):
    NT = TPC // 128
    xin = [fw.sb(f"xin{i}", [128, D], F32) for i in range(2)]
    for i in range(NT):
        t = xin[i % 2]
        fw.dma("sp", t[:], x_ext[HALO + i * 128:HALO + (i + 1) * 128, :], writes=[t])
        xb = c.xb[c.xb_i % 2]; c.xb_i += 1
        fw.op("pool", lambda e: e.tensor_copy(out=xb[:], in_=t[:]), reads=[t], writes=[xb])
        transpose_to_xT(fw, c, xb, xT, xT3[:, :, HALO + i * 128:HALO + (i + 1) * 128])
    for side in range(2):
        t = xin[side]
        r0 = 0 if side == 0 else TPC + HALO
        fw.dma("sp", t[0:HALO, :], x_ext[r0:r0 + HALO, :], writes=[t])
        xb = c.xb[c.xb_i % 2]; c.xb_i += 1
        fw.op("pool", lambda e: e.tensor_copy(out=xb[0:HALO, :], in_=t[0:HALO, :]), reads=[t], writes=[xb])

        def fn(e):
            inst = None
            for k in range(KC):
                inst = e.transpose(c.pT[:, k * HALO:(k + 1) * HALO], xb[0:HALO, k * 128:(k + 1) * 128], c.ident[0:HALO, 0:HALO])
            return inst
        fw.op("pe", fn, reads=[xb, c.ident], writes=[c.pT])
        fw.op("act", lambda e: e.copy(out=xT3[:, :, r0:r0 + HALO],
                                      in_=c.pT[:, 0:KC * HALO].rearrange("p (k n) -> p k n", k=KC)),
              reads=[c.pT], writes=[xT])


def fm_proj(fw, c, p, wv, wtl, j0, xT, xT3, c0, n):
    mm_group(fw, p, [(p[:, 0:n], wv[:, k, j0 * 128:(j0 + 1) * 128], xT3[:, k, c0:c0 + n], k == 0, k == KC - 1)
                     for k in range(KC)], reads=[wtl, xT])


def build_o1(TPC):
    NT = TPC // 128
    NQ = TPC // 512
    EXT = TPC + 2 * HALO
    nc = bass.Bass("TRN2", target_bir_lowering=False)
    x_ext = nc.dram_tensor("x_ext", [EXT, D], F32, kind="ExternalInput").ap()
    w_in = nc.dram_tensor("w_in", [D, 2048], BF16, kind="ExternalInput").ap()
    cs = nc.dram_tensor("cs", [2, 128, TPC], F32, kind="ExternalInput").ap()
    sp = nc.dram_tensor("sp", [128, 8 + 4 * 34], F32, kind="ExternalInput").ap()
    consts = nc.dram_tensor("consts", [128, 640], F32, kind="ExternalInput").ap()
    kT_o = nc.dram_tensor("kT", [2, 128, TPC], BF16, kind="ExternalOutput").ap()
    v_o = nc.dram_tensor("v", [TPC, 256], BF16, kind="ExternalOutput").ap()
    qT_o = nc.dram_tensor("qT", [4, 128, TPC], BF16, kind="ExternalOutput").ap()
    yd_o = nc.dram_tensor("yd", [4, 128, TPC], BF16, kind="ExternalOutput").ap()
    with ExitStack() as st:
        fw = FW(nc, st)
        c = Ctx()
        setup_common(fw, c, consts)
        xT = fw.sb("xT", [128, KC * EXT], BF16)
        xT3 = xT.t[:].rearrange("p (k n) -> p k n", k=KC)
        build_xT_ext(fw, c, x_ext, TPC, xT, xT3)
        spt = fw.sb("spt", [128, 8 + 4 * 34], F32)
        fw.dma("sp", spt[:], sp[:, :], writes=[spt])
        cos = fw.sb("cos", [128, TPC], F32)
        sin = fw.sb("sin", [128, TPC], F32)
        fw.dma("sp", cos[:], cs[0], writes=[cos])
        fw.dma("sp", sin[:], cs[1], writes=[sin])
        wp = WPool(fw, 3, KC * 512)
        A = [fw.sb(f"A{i}", [128, 512], F32) for i in range(2)]
        B = [fw.sb(f"B{i}", [128, 512], F32) for i in range(2)]
        Cc = [fw.sb(f"C{i}", [128, 512], F32) for i in range(2)]
        T2 = [fw.sb(f"T2{i}", [128, 512], F32) for i in range(2)]
        ob = [fw.sb(f"ob{i}", [128, 512], BF16) for i in range(3)]
        cnt = [0]
        ones_f = c.cst_f
        inv_hd = 1.0 / 128.0

        def rope_norm(p, gcol, tile, dst_dram):
            i = cnt[0] % 2
            o = ob[cnt[0] % 3]
            cnt[0] += 1
            a, b, cc, t2 = A[i], B[i], Cc[i], T2[i]
            fw.op("act", lambda e: e.activation(out=a[:], in_=p[:, :], func=AF.Copy, scale=spt[:, gcol:gcol + 1]),
                  reads=[p, spt], writes=[a])
            fw.op("act", lambda e: e.activation(out=b[:], in_=p[:, :], func=AF.Square), reads=[p], writes=[b])
            p2 = next_pS(c)
            mm_group(fw, p2, [(p2[:, :], c.cst_f[:, 128:256], b[:], True, True)], reads=[c.cst_f, b])
            p3 = next_pS(c)
            mm_group(fw, p3, [(p3[:, :], c.cst_f[:, 256:384], a[:], True, True)], reads=[c.cst_f, a])
            fw.op("act", lambda e: e.activation(out=cc[:], in_=p2[:, :], func=AF.Sqrt, bias=c.eps[:], scale=inv_hd),
                  reads=[p2, c.eps], writes=[cc])
            fw.op("dve", lambda e: e.reciprocal(out=cc[:], in_=cc[:]), reads=[cc], writes=[cc])
            sl = slice(tile * 512, (tile + 1) * 512)
            fw.op("dve", lambda e: e.tensor_tensor(out=a[:], in0=a[:], in1=cos[:, sl], op=ALU.mult), reads=[a, cos], writes=[a])
            fw.op("dve", lambda e: e.tensor_tensor(out=t2[:], in0=p3[:, :], in1=sin[:, sl], op=ALU.mult), reads=[p3, sin], writes=[t2])
            fw.op("pool", lambda e: e.tensor_tensor(out=a[:], in0=a[:], in1=t2[:], op=ALU.add), reads=[a, t2], writes=[a])
            fw.op("dve", lambda e: e.tensor_tensor(out=o[:], in0=a[:], in1=cc[:], op=ALU.mult), reads=[a, cc], writes=[o])
            fw.dma("sp", dst_dram[:, sl], o[:], reads=[o])

        wtl, wv = load_w_kcn(fw, wp, w_in, 0, 512)
        for h in range(4):
            for q in range(NQ):
                p = next_pS(c)
                fm_proj(fw, c, p, wv, wtl, h, xT, xT3, HALO + q * 512, 512)
                rope_norm(p, 0, q, qT_o[h])
        wtl, wv = load_w_kcn(fw, wp, w_in, 512, 512)
        for h in range(2):
            for q in range(NQ):
                p = next_pS(c)
                fm_proj(fw, c, p, wv, wtl, h, xT, xT3, HALO + q * 512, 512)
                rope_norm(p, 1, q, kT_o[h])
        vb = [fw.sb(f"vb{i}", [128, 256], BF16) for i in range(2)]
        for nt in range(NT):
            p = next_pS(c)
            mm_group(fw, p, [(p[:, 0:256], xT3[:, k, HALO + nt * 128:HALO + (nt + 1) * 128], wv[:, k, 256:512], k == 0, k == KC - 1)
                             for k in range(KC)], reads=[wtl, xT])
            t = vb[nt % 2]
            fw.op("act", lambda e: e.copy(out=t[:], in_=p[:, 0:256]), reads=[p], writes=[t])
            fw.dma("sp", v_o[nt * 128:(nt + 1) * 128, :], t[:], reads=[t])
        wval = load_w_kcn(fw, wp, w_in, 1024, 512)
        wgat = load_w_kcn(fw, wp, w_in, 1536, 512)
        W = 512 + 2 * HALO
        u = [fw.sb(f"u{i}", [128, W], F32) for i in range(2)]
        sgm = [fw.sb(f"sgm{i}", [128, W], F32) for i in range(2)]
        acc = fw.sb("acc", [128, 4 * 512], F32)
        acc3 = acc.t[:].rearrange("p (c n) -> p c n", c=4)
        sq = [fw.sb(f"sq{i}", [128, 512], F32) for i in range(2)]
        mean = fw.sb("mean", [128, 512], F32)
        rstd = fw.sb("rstd", [128, 512], F32)
        for q in range(NQ):
            c0 = q * 512
            for cc in range(4):
                ut, sg = u[cc % 2], sgm[cc % 2]
                for (o0, n) in ((0, 512), (512, 2 * HALO)):
                    pg = next_pS(c)
                    fm_proj(fw, c, pg, wgat[1], wgat[0], cc, xT, xT3, c0 + o0, n)
                    fw.op("act", lambda e: e.activation(out=sg[:, o0:o0 + n], in_=pg[:, 0:n], func=AF.Sigmoid), reads=[pg], writes=[sg])
                    pv = next_pS(c)
                    fm_proj(fw, c, pv, wval[1], wval[0], cc, xT, xT3, c0 + o0, n)
                    fw.op("dve", lambda e: e.tensor_tensor(out=ut[:, o0:o0 + n], in0=pv[:, 0:n], in1=sg[:, o0:o0 + n], op=ALU.mult),
                          reads=[pv, sg], writes=[ut])
                base = 8 + cc * 34
                eng = "dve"
                fw.op(eng, lambda e: e.tensor_scalar(out=acc3[:, cc, :], in0=ut[:, 1:513], scalar1=spt[:, base:base + 1],
                                                     scalar2=spt[:, base + 31:base + 32], op0=ALU.mult, op1=ALU.add),
                      reads=[ut, spt], writes=[acc])
                for j in range(1, 31):
                    fw.op(eng, lambda e: e.scalar_tensor_tensor(out=acc3[:, cc, :], in0=ut[:, j + 1:j + 513],
                                                                scalar=spt[:, base + j:base + j + 1], in1=acc3[:, cc, :],
                                                                op0=ALU.mult, op1=ALU.add),
                          reads=[ut, spt, acc], writes=[acc])
            ps1 = next_pS(c)
            mm_group(fw, ps1, [(ps1[:, :], c.cst_f[:, 128:256], acc3[:, cc, :], cc == 0, cc == 3) for cc in range(4)],
                     reads=[c.cst_f, acc])
            ps2 = next_pS(c)
            for cc in range(4):
                s_ = sq[cc % 2]
                fw.op("act", lambda e: e.activation(out=s_[:], in_=acc3[:, cc, :], func=AF.Square), reads=[acc], writes=[s_])
                mm_group(fw, ps2, [(ps2[:, :], c.cst_f[:, 128:256], s_[:], cc == 0, cc == 3)], reads=[c.cst_f, s_])
            fw.op("act", lambda e: e.activation(out=mean[:], in_=ps1[:, :], func=AF.Copy, scale=1.0 / 512.0), reads=[ps1], writes=[mean])
            fw.op("dve", lambda e: e.tensor_tensor(out=rstd[:], in0=mean[:], in1=mean[:], op=ALU.mult), reads=[mean], writes=[rstd])
            fw.op("dve", lambda e: e.scalar_tensor_tensor(out=rstd[:], in0=ps2[:, :], scalar=1.0 / 512.0, in1=rstd[:],
                                                          op0=ALU.mult, op1=ALU.subtract), reads=[ps2, rstd], writes=[rstd])
            fw.op("act", lambda e: e.activation(out=rstd[:], in_=rstd[:], func=AF.Sqrt, bias=c.eps[:], scale=1.0),
                  reads=[rstd, c.eps], writes=[rstd])
            fw.op("dve", lambda e: e.reciprocal(out=rstd[:], in_=rstd[:]), reads=[rstd], writes=[rstd])
            for cc in range(4):
                base = 8 + cc * 34
                s_ = sq[cc % 2]
                o = ob[cnt[0] % 3]; cnt[0] += 1
                fw.op("dve", lambda e: e.tensor_tensor(out=s_[:], in0=acc3[:, cc, :], in1=mean[:], op=ALU.subtract), reads=[acc, mean], writes=[s_])
                fw.op("pool", lambda e: e.tensor_tensor(out=s_[:], in0=s_[:], in1=rstd[:], op=ALU.mult), reads=[s_, rstd], writes=[s_])
                fw.op("act", lambda e: e.activation(out=s_[:], in_=s_[:], func=AF.Identity, scale=spt[:, base + 32:base + 33],
                                                    bias=spt[:, base + 33:base + 34]), reads=[s_, spt], writes=[s_])
                fw.op("act", lambda e: e.activation(out=o[:], in_=s_[:], func=AF.Silu), reads=[s_], writes=[o])
                fw.dma("sp", yd_o[cc][:, q * 512:(q + 1) * 512], o[:], reads=[o])
        fw.finish()
    return nc


def build_o2(TPC, S, PK=2048):
    NT = TPC // 128
    NQ = TPC // 512
    NP = S // PK
    KB = PK // 128
    nc = bass.Bass("TRN2", target_bir_lowering=False)
    x_in = nc.dram_tensor("x", [TPC, D], F32, kind="ExternalInput").ap()
    qT_i = nc.dram_tensor("qT", [4, 128, TPC], BF16, kind="ExternalInput").ap()
    yd_i = nc.dram_tensor("yd", [4, 128, TPC], BF16, kind="ExternalInput").ap()
    kT_i = nc.dram_tensor("kT", [2, 128, S], BF16, kind="ExternalInput").ap()
    v_i = nc.dram_tensor("v", [S, 256], BF16, kind="ExternalInput").ap()
    w_out = nc.dram_tensor("w_out", [D, D], BF16, kind="ExternalInput").ap()
    lnp = nc.dram_tensor("lnp", [2, D], F32, kind="ExternalInput").ap()
    consts = nc.dram_tensor("consts", [128, 640], F32, kind="ExternalInput").ap()
    x_out = nc.dram_tensor("y", [TPC, D], F32, kind="ExternalOutput").ap()
    scale = 1.0 / np.sqrt(128.0)
    with ExitStack() as st:
        fw = FW(nc, st)
        c = Ctx()
        setup_common(fw, c, consts)
        yab = fw.sb("yab", [128, KC * TPC], BF16)
        yab3 = yab.t[:].rearrange("p (k n) -> p k n", k=KC)
        for cc in range(4):
            fw.dma("sp", yab3[:, 4 + cc, :], yd_i[cc], writes=[yab])
        qT = fw.sb("qT", [128, 4 * TPC], BF16)
        qT3 = qT.t[:].rearrange("p (k n) -> p k n", k=4)
        for h in range(4):
            fw.dma("sp", qT3[:, h, :], qT_i[h], writes=[qT])
        wo_sb = fw.sb("wo_sb", [128, KC * D], BF16)
        wo3 = wo_sb.t[:].rearrange("p (k n) -> p k n", k=KC)
        fw.dma("sp", wo3, w_out.rearrange("(k p) n -> p k n", p=128), writes=[wo_sb])
        lnb = [fw.sb(f"lnb{i}", [128, D], F32) for i in range(2)]
        for i in range(2):
            load_bc(fw, lnb[i], lnp[i, :])
        kbuf = [fw.sb(f"kbuf{i}", [128, PK], BF16) for i in range(2)]
        vbuf = [fw.sb(f"vbuf{i}", [128, KB * 128], BF16) for i in range(2)]
        oacc = fw.sb("oacc", [128, 2 * NQ * 512], F32)
        sacc = fw.sb("sacc", [128, 2 * NQ * 512], F32)
        pts = [fw.sb(f"pts{i}", [128, 512], BF16) for i in range(4)]
        rr = fw.sb("rr", [128, 512], F32)
        pi = 0
        it = 0
        for hk in range(2):
            for pc in range(NP):
                kb_t, vb_t = kbuf[it % 2], vbuf[it % 2]
                it += 1
                fw.dma("sp", kb_t[:], kT_i[hk][:, pc * PK:(pc + 1) * PK], writes=[kb_t])
                fw.dma("sp", vb_t.t[:].rearrange("p (b d) -> p b d", b=KB),
                       v_i[pc * PK:(pc + 1) * PK, hk * 128:(hk + 1) * 128].rearrange("(b p) d -> p b d", p=128), writes=[vb_t])
                for hl in range(2):
                    h = 2 * hk + hl
                    for q in range(NQ):
                        py = next_pY(c)
                        for kb in range(KB):
                            ps_ = next_pS(c)
                            mm_group(fw, ps_, [(ps_[:, :], kb_t[:, kb * 128:(kb + 1) * 128], qT3[:, h, q * 512:(q + 1) * 512], True, True)],
                                     reads=[kb_t, qT])
                            pt = pts[pi % 4]; pi += 1
                            fw.op("act", lambda e: e.activation(out=pt[:], in_=ps_[:, :], func=AF.Exp, scale=scale), reads=[ps_], writes=[pt])
                            mm_group(fw, py, [(py[:, 0:512], vb_t[:, kb * 128:(kb + 1) * 128], pt[:], kb == 0, kb == KB - 1),
                                              (py[:, 512:1024], c.ones_bf[:], pt[:], kb == 0, kb == KB - 1)],
                                     reads=[vb_t, pt, c.ones_bf])
                        sl = slice((hl * NQ + q) * 512, (hl * NQ + q + 1) * 512)
                        if pc == 0:
                            fw.op("dve", lambda e: e.tensor_copy(out=oacc[:, sl], in_=py[:, 0:512]), reads=[py], writes=[oacc])
                            fw.op("dve", lambda e: e.tensor_copy(out=sacc[:, sl], in_=py[:, 512:1024]), reads=[py], writes=[sacc])
                        else:
                            fw.op("dve", lambda e: e.tensor_tensor(out=oacc[:, sl], in0=oacc[:, sl], in1=py[:, 0:512], op=ALU.add),
                                  reads=[py, oacc], writes=[oacc])
                            fw.op("dve", lambda e: e.tensor_tensor(out=sacc[:, sl], in0=sacc[:, sl], in1=py[:, 512:1024], op=ALU.add),
                                  reads=[py, sacc], writes=[sacc])
            for hl in range(2):
                h = 2 * hk + hl
                for q in range(NQ):
                    sl = slice((hl * NQ + q) * 512, (hl * NQ + q + 1) * 512)
                    fw.op("dve", lambda e: e.reciprocal(out=rr[:], in_=sacc[:, sl]), reads=[sacc], writes=[rr])
                    fw.op("dve", lambda e: e.tensor_tensor(out=yab3[:, h, q * 512:(q + 1) * 512], in0=oacc[:, sl], in1=rr[:], op=ALU.mult),
                          reads=[oacc, rr], writes=[yab])
        xr = [fw.sb(f"xr{i}", [128, D], F32) for i in range(3)]
        for nt in range(NT):
            t = xr[nt % 3]
            fw.dma("sp", t[:], x_in[nt * 128:(nt + 1) * 128, :], writes=[t])
            py = next_pY(c)
            mm_group(fw, py, [(py[:, hf * 512:(hf + 1) * 512], yab3[:, k, nt * 128:(nt + 1) * 128],
                               wo3[:, k, hf * 512:(hf + 1) * 512], k == 0, k == KC - 1)
                              for hf in range(2) for k in range(KC)], reads=[yab, wo_sb])
            ln_epilogue(fw, c, t, py, lnb[0], lnb[1])
            fw.dma("sp", x_out[nt * 128:(nt + 1) * 128, :], t[:], reads=[t])
        fw.finish()
    return nc


def host_consts3():
    a = np.zeros((128, 2176), np.float32)
    a[:, 0:640] = host_consts2()
    m = np.ones(512, np.float32)
    m[::64] = 0.0
    a[:, 640:1152] = m[None, :]
    a[:, 1152:1664] = np.tile(a[:, 384:512], (1, 4))
    a[:, 1664:2176] = np.tile(a[:, 512:640], (1, 4))
    return a


class _Stop(Exception):
    pass


def build_e(TPC, layer, state_only, dbg=0):
    NT = TPC // 128
    NG = TPC // 512
    EXT = TPC + 2 * HALO
    nc = bass.Bass("TRN2", target_bir_lowering=False)
    x_ext = nc.dram_tensor("x_ext", [EXT, D], F32, kind="ExternalInput").ap()
    w_in = nc.dram_tensor("w_in", [D, 3072], BF16, kind="ExternalInput").ap()
    lbl = nc.dram_tensor("lbl", [128, 32], F32, kind="ExternalInput").ap()
    consts = nc.dram_tensor("consts", [128, 2176], F32, kind="ExternalInput").ap()
    if state_only:
        S_o = nc.dram_tensor("S_loc", [8, 128, 128], F32, kind="ExternalOutput").ap()
        L_o = nc.dram_tensor("logD", [128, 8], F32, kind="ExternalOutput").ap()
    else:
        w_out = nc.dram_tensor("w_out", [D, D], BF16, kind="ExternalInput").ap()
        pw = nc.dram_tensor("pool_w", [4, 128, 128], BF16, kind="ExternalInput").ap()
        sp = nc.dram_tensor("sp", [128, 8], F32, kind="ExternalInput").ap()
        ic = nc.dram_tensor("ic", [4, TPC], F32, kind="ExternalInput").ap()
        S_d = nc.dram_tensor("S_d", [8, 7, 128, 128], F32, kind="ExternalInput").ap()
        L_d = nc.dram_tensor("L_d", [128, 56], F32, kind="ExternalInput").ap()
        lnp = nc.dram_tensor("lnp", [2, D], F32, kind="ExternalInput").ap()
        x_out = nc.dram_tensor("y", [TPC, D], F32, kind="ExternalOutput").ap()
    with ExitStack() as st:
        fw = FW(nc, st)
        c = Ctx()

        def chk(n):
            if dbg == n:
                fw.finish()
                raise _Stop()
        try:
            _build_e_body(fw, c, nc, locals(), chk)
        except _Stop:
            if getattr(c, 'ph_h', None) is not None:
                c.ph_h.close()
    return nc


def _build_e_body(fw, c, nc, L, chk):
    TPC, layer, state_only = L['TPC'], L['layer'], L['state_only']
    NT, NG, EXT = L['NT'], L['NG'], L['EXT']
    x_ext, w_in, lbl, consts = L['x_ext'], L['w_in'], L['lbl'], L['consts']
    if state_only:
        S_o, L_o = L['S_o'], L['L_o']
    else:
        w_out, pw, sp, ic, S_d, L_d, lnp, x_out = (L[k] for k in ('w_out', 'pw', 'sp', 'ic', 'S_d', 'L_d', 'lnp', 'x_out'))
    if True:
        setup_common(fw, c, consts, ncols=2176)
        m01 = c.cst_f
        xT = fw.sb("xT", [128, KC * EXT], BF16)
        xT3 = xT.t[:].rearrange("p (k n) -> p k n", k=KC)
        build_xT_ext(fw, c, x_ext, TPC, xT, xT3)
        wp = WPool(fw, 6, KC * 128)
        lb_t = fw.sb("lb_t", [128, 32], F32)
        fw.dma("sp", lb_t[:], lbl[:, :], writes=[lb_t])
        fw.op("act", lambda e: e.activation(out=lb_t[:], in_=lb_t[:], func=AF.Exp), reads=[lb_t], writes=[lb_t])
        lb3 = lb_t.t[:].rearrange("p (d l h) -> p d l h", d=2, l=4)
        tot = fw.sb("tot", [128, 8], F32)
        num = fw.sb("num", [128, 8], F32)
        oml = fw.sb("oml", [128, 8], F32)
        tot3 = tot.t[:].rearrange("p (d h) -> p d h", d=2)
        num3 = num.t[:].rearrange("p (d h) -> p d h", d=2)
        fw.op("dve", lambda e: e.tensor_tensor(out=tot3, in0=lb3[:, :, 0, :], in1=lb3[:, :, 1, :], op=ALU.add), reads=[lb_t], writes=[tot])
        fw.op("dve", lambda e: e.tensor_tensor(out=tot3, in0=tot3, in1=lb3[:, :, 2, :], op=ALU.add), reads=[lb_t, tot], writes=[tot])
        fw.op("dve", lambda e: e.tensor_tensor(out=tot3, in0=tot3, in1=lb3[:, :, 3, :], op=ALU.add), reads=[lb_t, tot], writes=[tot])
        fw.op("dve", lambda e: e.memset(num[:], 0.0), writes=[num])
        for l_ in range(1, layer + 1):
            fw.op("dve", lambda e: e.tensor_tensor(out=num3, in0=num3, in1=lb3[:, :, l_, :], op=ALU.add), reads=[lb_t, num], writes=[num])
        fw.op("dve", lambda e: e.reciprocal(out=tot[:], in_=tot[:]), reads=[tot], writes=[tot])
        fw.op("dve", lambda e: e.tensor_tensor(out=num[:], in0=num[:], in1=tot[:], op=ALU.mult), reads=[num, tot], writes=[num])
        fw.op("dve", lambda e: e.tensor_scalar(out=oml[:], in0=num[:], scalar1=-1.0, scalar2=1.0, op0=ALU.mult, op1=ALU.add),
              reads=[num], writes=[oml])
        chk(1)
        stA = [fw.sb(f"stA{q}", [128, 128], F32) for q in range(8)]
        stB = [fw.sb(f"stB{q}", [128, 128], F32) for q in range(8)]
        if state_only:
            for q in range(8):
                fw.op("pool", lambda e: e.memset(stA[q][:], 0.0), writes=[stA[q]])
            logD = fw.sb("logD", [128, 8], F32)
            fw.op("pool", lambda e: e.memset(logD[:], 0.0), writes=[logD])
            ldt = fw.sb("ldt", [128, 1], F32)
        else:
            Ld = fw.sb("Ld", [128, 56], F32)
            fw.dma("sp", Ld[:], L_d[:, :], writes=[Ld])
            Aw = fw.sb("Aw", [128, 56], F32)
            Ld3 = Ld.t[:].rearrange("p (q j) -> p q j", j=7)
            Aw3 = Aw.t[:].rearrange("p (q j) -> p q j", j=7)
            fw.op("dve", lambda e: e.memset(Aw[:], 0.0), writes=[Aw])
            for j_ in range(1, 7):
                fw.op("dve", lambda e: e.tensor_tensor(out=Aw3[:, :, j_], in0=Aw3[:, :, j_ - 1], in1=Ld3[:, :, j_ - 1], op=ALU.add),
                      reads=[Aw, Ld], writes=[Aw])
            fw.op("act", lambda e: e.activation(out=Aw[:], in_=Aw[:], func=AF.Exp), reads=[Aw], writes=[Aw])
            sdb = [fw.sb(f"sdb{i}", [128, 128], F32) for i in range(3)]
            si = 0
            for q in range(8):
                for j_ in range(7):
                    t = sdb[si % 3]; si += 1
                    fw.dma("sp", t[:], S_d[q, j_], writes=[t])
                    if j_ == 0:
                        fw.op("dve", lambda e: e.tensor_scalar(out=stA[q][:], in0=t[:], scalar1=Aw[:, q * 7:q * 7 + 1], scalar2=None, op0=ALU.mult),
                              reads=[t, Aw], writes=[stA[q]])
                    else:
                        fw.op("dve", lambda e: e.scalar_tensor_tensor(out=stA[q][:], in0=t[:], scalar=Aw[:, q * 7 + j_:q * 7 + j_ + 1],
                                                                      in1=stA[q][:], op0=ALU.mult, op1=ALU.add),
                              reads=[t, Aw, stA[q]], writes=[stA[q]])
            yab = fw.sb("yab", [128, KC * TPC], BF16)
            yab3 = yab.t[:].rearrange("p (k n) -> p k n", k=KC)
            obst = fw.sb("obst", [128, 4 * TPC], F32)
            obst3 = obst.t[:].rearrange("p (h n) -> p h n", h=4)
            spt = fw.sb("spt", [128, 8], F32)
            fw.dma("sp", spt[:], sp[:, :], writes=[spt])
        chk(2)
        ph_h = ExitStack()
        c.ph_h = ph_h
        A = fw.sb("A", [128, 512], F32, ph_h)
        Bf = fw.sb("Bf", [128, 512], F32, ph_h)
        Cm = fw.sb("Cm", [128, 512], F32, ph_h)
        Bb = fw.sb("Bb", [128, 512], F32, ph_h)
        E2 = fw.sb("E2", [128, 512], F32, ph_h)
        Kt = fw.sb("Kt", [128, 512], F32, ph_h)
        ed = fw.sb("ed", [128, 8], F32, ph_h)
        kdT = fw.sb("kdT", [128, 512], BF16, ph_h)
        kd_tok = fw.sb("kd_tok", [128, 512], BF16, ph_h)
        v_tok = fw.sb("v_tok", [128, 512], BF16, ph_h)
        Cm3 = Cm.t[:].rearrange("p (c t) -> p c t", t=64)
        if not state_only:
            E1 = fw.sb("E1", [128, 512], F32, ph_h)
            Q = fw.sb("Q", [128, 512], F32, ph_h)
            qtT = fw.sb("qtT", [128, 512], BF16, ph_h)
            ktT = fw.sb("ktT", [128, 512], BF16, ph_h)
            sT = fw.sb("sT", [128, 512], BF16, ph_h)
            S_bf = fw.sb("S_bf", [128, 8 * 128], BF16, ph_h)
            Of = fw.sb("Of", [128, 512], F32, ph_h)
            Sq = fw.sb("Sq", [128, 512], F32, ph_h)
            Rs = fw.sb("Rs", [128, 512], F32, ph_h)
            Og = fw.sb("Og", [128, 512], F32, ph_h)

        for dr in (1, 0):
            fwd = dr == 0
            for h in range(4):
                qi = dr * 4 + h
                wz = load_w_kcn(fw, wp, w_in, (8 + dr * 4 + h) * 128, 128)
                wi = load_w_kcn(fw, wp, w_in, (4 + h) * 128, 128)
                if not state_only:
                    wq_ = load_w_kcn(fw, wp, w_in, h * 128, 128)
                    if fwd:
                        wog = load_w_kcn(fw, wp, w_in, (16 + h) * 128, 128)
                cur, nxt = stA[qi], stB[qi]
                groups = range(NG) if fwd else range(NG - 1, -1, -1)
                for g in groups:
                    c0 = HALO + g * 512
                    pz = next_pS(c)
                    fm_proj(fw, c, pz, wz[1], wz[0], 0, xT, xT3, c0, 512)
                    fw.op("act", lambda e: e.activation(out=A[:], in_=pz[:, :], func=AF.Sigmoid, scale=-1.0), reads=[pz], writes=[A])
                    fw.op("dve", lambda e: e.tensor_scalar(out=A[:], in0=A[:], scalar1=oml[:, qi:qi + 1], scalar2=None, op0=ALU.mult),
                          reads=[A, oml], writes=[A])
                    fw.op("act", lambda e: e.activation(out=Bf[:], in_=A[:], func=AF.Ln, scale=-1.0, bias=1.0), reads=[A], writes=[Bf])
                    fw.op("dve", lambda e: e.tensor_tensor_scan(out=Cm[:], data0=m01[:, 640:1152], data1=Bf[:], initial=0.0,
                                                                op0=ALU.mult, op1=ALU.add), reads=[m01, Bf], writes=[Cm])
                    chk(3)
                    if fwd:
                        bsrc = Cm
                    else:
                        fw.op("dve", lambda e: e.tensor_tensor(out=Bf[:], in0=Bf[:], in1=Cm[:], op=ALU.subtract), reads=[Bf, Cm], writes=[Bf])
                        for cch in range(8):
                            sl = slice(cch * 64, (cch + 1) * 64)
                            fw.op("pool", lambda e: e.tensor_scalar(out=Bb[:, sl], in0=Bf[:, sl], scalar1=Cm[:, cch * 64 + 63:cch * 64 + 64],
                                                                    scalar2=None, op0=ALU.add), reads=[Bf, Cm], writes=[Bb])
                        bsrc = Bb
                    fw.op("act", lambda e: e.activation(out=E2[:], in_=bsrc[:], func=AF.Exp, scale=-1.0), reads=[bsrc], writes=[E2])
                    fw.op("act", lambda e: e.activation(out=ed[:], in_=Cm3[:, :, 63], func=AF.Exp), reads=[Cm], writes=[ed])
                    fw.op("dve", lambda e: e.tensor_tensor(out=Kt[:], in0=A[:], in1=E2[:], op=ALU.mult), reads=[A, E2], writes=[Kt])
                    for cch in range(8):
                        sl = slice(cch * 64, (cch + 1) * 64)
                        fw.op("pool", lambda e: e.tensor_scalar(out=kdT[:, sl], in0=Kt[:, sl], scalar1=ed[:, cch:cch + 1], scalar2=None,
                                                                op0=ALU.mult), reads=[Kt, ed], writes=[kdT])

                    chk(4)

                    def fn(e):
                        inst = None
                        for b_ in range(4):
                            inst = e.transpose(c.pT[:, b_ * 128:(b_ + 1) * 128], kdT[:, b_ * 128:(b_ + 1) * 128], c.ident[:])
                        return inst
                    fw.op("pe", fn, reads=[kdT, c.ident], writes=[c.pT])
                    fw.op("act", lambda e: e.copy(out=kd_tok[:], in_=c.pT[:, 0:512]), reads=[c.pT], writes=[kd_tok])
                    pv = next_pS(c)
                    mm_group(fw, pv, [(pv[:, b_ * 128:(b_ + 1) * 128], xT3[:, k, c0 + b_ * 128:c0 + (b_ + 1) * 128], wi[1][:, k, 0:128],
                                       k == 0, k == KC - 1) for b_ in range(4) for k in range(KC)], reads=[wi[0], xT])
                    fw.op("act", lambda e: e.copy(out=v_tok[:], in_=pv[:, :]), reads=[pv], writes=[v_tok])
                    pU = next_pY(c)
                    items = []
                    for cch in range(8):
                        b_, r0 = cch // 2, (cch % 2) * 64
                        uc = (cch % 2) * 512 + (cch // 2) * 128
                        items.append((pU[:, uc:uc + 128], kd_tok[r0:r0 + 64, b_ * 128:(b_ + 1) * 128],
                                      v_tok[r0:r0 + 64, b_ * 128:(b_ + 1) * 128], True, True))
                    mm_group(fw, pU, items, reads=[kd_tok, v_tok])
                    chk(5)
                    if state_only:
                        fw.op("dve", lambda e: e.tensor_reduce(out=ldt[:], in_=Cm3[:, :, 63], axis=AX.X, op=ALU.add), reads=[Cm], writes=[ldt])
                        fw.op("dve", lambda e: e.tensor_tensor(out=logD[:, qi:qi + 1], in0=logD[:, qi:qi + 1], in1=ldt[:], op=ALU.add),
                              reads=[logD, ldt], writes=[logD])
                    chk(6)
                    order = range(8) if fwd else range(7, -1, -1)
                    for cch in order:
                        if not state_only:
                            fw.op("pool", lambda e: e.tensor_copy(out=S_bf[:, cch * 128:(cch + 1) * 128], in_=cur[:]), reads=[cur], writes=[S_bf])
                        fw.op("dve", lambda e: e.scalar_tensor_tensor(out=nxt[:], in0=cur[:], scalar=ed[:, cch:cch + 1],
                                                                      in1=pU[:, (cch % 2) * 512 + (cch // 2) * 128:(cch % 2) * 512 + (cch // 2) * 128 + 128],
                                                                      op0=ALU.mult, op1=ALU.add),
                              reads=[cur, ed, pU], writes=[nxt])
                        cur, nxt = nxt, cur
                    chk(7)
                    if state_only:
                        continue
                    fw.op("act", lambda e: e.activation(out=E1[:], in_=bsrc[:], func=AF.Exp), reads=[bsrc], writes=[E1])
                    pq = next_pS(c)
                    fm_proj(fw, c, pq, wq_[1], wq_[0], 0, xT, xT3, c0, 512)
                    fw.op("act", lambda e: e.activation(out=Q[:], in_=pq[:, :], func=AF.Silu), reads=[pq], writes=[Q])
                    fw.op("dve", lambda e: e.tensor_tensor(out=qtT[:], in0=Q[:], in1=E1[:], op=ALU.mult), reads=[Q, E1], writes=[qtT])
                    fw.op("pool", lambda e: e.tensor_copy(out=ktT[:], in_=Kt[:]), reads=[Kt], writes=[ktT])
                    psc = next_pS(c)
                    mm_group(fw, psc, [(psc[:, b_ * 128:(b_ + 1) * 128], ktT[:, b_ * 128:(b_ + 1) * 128], qtT[:, b_ * 128:(b_ + 1) * 128], True, True)
                                       for b_ in range(4)], reads=[ktT, qtT])
                    moff = 1152 if fwd else 1664
                    fw.op("dve", lambda e: e.tensor_tensor(out=sT[:], in0=psc[:, :], in1=c.cst_f[:, moff:moff + 512], op=ALU.mult),
                          reads=[psc, c.cst_f], writes=[sT])
                    po = next_pS(c)
                    items = []
                    for cch in range(8):
                        b_ = cch // 2
                        cs_ = slice(cch * 64, (cch + 1) * 64)
                        items.append((po[:, cs_], S_bf[:, cch * 128:(cch + 1) * 128], qtT[:, cs_], True, False))
                        items.append((po[:, cs_], v_tok[:, b_ * 128:(b_ + 1) * 128], sT[:, cs_], False, True))
                    mm_group(fw, po, items, reads=[v_tok, sT, S_bf, qtT])
                    gs = slice(g * 512, (g + 1) * 512)
                    if not fwd:
                        fw.op("act", lambda e: e.copy(out=obst3[:, h, gs], in_=po[:, :]), reads=[po], writes=[obst])
                    else:
                        fw.op("dve", lambda e: e.tensor_tensor(out=Of[:], in0=po[:, :], in1=obst3[:, h, gs], op=ALU.add), reads=[po, obst], writes=[Of])
                        fw.op("act", lambda e: e.activation(out=Sq[:], in_=Of[:], func=AF.Square), reads=[Of], writes=[Sq])
                        pss = next_pS(c)
                        mm_group(fw, pss, [(pss[:, :], c.cst_f[:, 128:256], Sq[:], True, True)], reads=[c.cst_f, Sq])
                        fw.op("act", lambda e: e.activation(out=Rs[:], in_=pss[:, :], func=AF.Sqrt, bias=c.eps[:], scale=1.0 / 128.0),
                              reads=[pss, c.eps], writes=[Rs])
                        fw.op("dve", lambda e: e.reciprocal(out=Rs[:], in_=Rs[:]), reads=[Rs], writes=[Rs])
                        pog = next_pS(c)
                        fm_proj(fw, c, pog, wog[1], wog[0], 0, xT, xT3, c0, 512)
                        fw.op("act", lambda e: e.activation(out=Og[:], in_=pog[:, :], func=AF.Silu), reads=[pog], writes=[Og])
                        fw.op("dve", lambda e: e.tensor_tensor(out=Of[:], in0=Of[:], in1=Rs[:], op=ALU.mult), reads=[Of, Rs], writes=[Of])
                        fw.op("dve", lambda e: e.scalar_tensor_tensor(out=yab3[:, h, gs], in0=Of[:], scalar=spt[:, h:h + 1], in1=Og[:],
                                                                      op0=ALU.mult, op1=ALU.mult), reads=[Of, spt, Og], writes=[yab])
                if state_only:
                    fw.dma("sp", S_o[qi], cur[:], reads=[cur])
        if state_only:
            fw.dma("sp", L_o[:, :], logD[:], reads=[logD])
            fw.finish()
            ph_h.close()
            return
        fw.barrier()
        ph_h.close()
        W = 512 + 2 * HALO
        U = fw.sb("U", [128, W], F32)
        s2 = fw.sb("s2", [128, W], F32)
        s4 = fw.sb("s4", [128, W], F32)
        icb = fw.sb("icb", [128, 512], F32)
        dT = fw.sb("dT", [128, 512], BF16)
        pw_sb = fw.sb("pw_sb", [128, 4 * 128], BF16)
        fw.dma("sp", pw_sb.t[:].rearrange("p (g d) -> p g d", g=4), pw.rearrange("g c d -> c g d"), writes=[pw_sb])
        for gg in range(4):
            wu = load_w_kcn(fw, wp, w_in, (20 + gg) * 128, 128)
            for g in range(NG):
                c0 = g * 512
                for (o0, n) in ((0, 512), (512, 2 * HALO)):
                    pu = next_pS(c)
                    fm_proj(fw, c, pu, wu[1], wu[0], 0, xT, xT3, c0 + o0, n)
                    fw.op("act", lambda e: e.copy(out=U[:, o0:o0 + n], in_=pu[:, 0:n]), reads=[pu], writes=[U])
                fw.dma("sp", icb[:], ic[gg, g * 512:(g + 1) * 512].partition_broadcast(128), writes=[icb])
                fw.op("dve", lambda e: e.tensor_tensor(out=s2[:, 1:W], in0=U[:, 0:W - 1], in1=U[:, 1:W], op=ALU.add), reads=[U], writes=[s2])
                src = s2
                if gg >= 1:
                    fw.op("dve", lambda e: e.tensor_tensor(out=s4[:, 2:W - 1], in0=s2[:, 1:W - 2], in1=s2[:, 3:W], op=ALU.add), reads=[s2], writes=[s4])
                    src = s4
                if gg >= 2:
                    fw.op("dve", lambda e: e.tensor_tensor(out=s2[:, 4:W - 3], in0=s4[:, 2:W - 5], in1=s4[:, 6:W - 1], op=ALU.add), reads=[s4], writes=[s2])
                    src = s2
                if gg >= 3:
                    fw.op("dve", lambda e: e.tensor_tensor(out=s4[:, 8:W - 7], in0=s2[:, 4:W - 11], in1=s2[:, 12:W - 3], op=ALU.add), reads=[s2], writes=[s4])
                    src = s4
                fw.op("dve", lambda e: e.tensor_tensor(out=icb[:], in0=src[:, HALO:HALO + 512], in1=icb[:], op=ALU.mult), reads=[src, icb], writes=[icb])
                fw.op("dve", lambda e: e.tensor_tensor(out=dT[:], in0=icb[:], in1=U[:, HALO:HALO + 512], op=ALU.subtract), reads=[icb, U], writes=[dT])
                pb = next_pS(c)
                mm_group(fw, pb, [(pb[:, :], pw_sb[:, gg * 128:(gg + 1) * 128], dT[:], True, True)], reads=[pw_sb, dT])
                fw.op("act", lambda e: e.activation(out=yab3[:, 4 + gg, g * 512:(g + 1) * 512], in_=pb[:, :], func=AF.Copy,
                                                    scale=spt[:, 4 + gg:5 + gg]), reads=[pb, spt], writes=[yab])
        wo_sb = fw.sb("wo_sb", [128, KC * D], BF16)
        wo3 = wo_sb.t[:].rearrange("p (k n) -> p k n", k=KC)
        fw.dma("sp", wo3, w_out.rearrange("(k p) n -> p k n", p=128), writes=[wo_sb])
        lnb = [fw.sb(f"lnb{i}", [128, D], F32) for i in range(2)]
        for i in range(2):
            load_bc(fw, lnb[i], lnp[i, :])
        xr = [fw.sb(f"xr{i}", [128, D], F32) for i in range(3)]
        for nt in range(NT):
            t = xr[nt % 3]
            fw.dma("sp", t[:], x_ext[HALO + nt * 128:HALO + (nt + 1) * 128, :], writes=[t])
            py = next_pY(c)
            mm_group(fw, py, [(py[:, hf * 512:(hf + 1) * 512], yab3[:, k, nt * 128:(nt + 1) * 128],
                               wo3[:, k, hf * 512:(hf + 1) * 512], k == 0, k == KC - 1)
                              for hf in range(2) for k in range(KC)], reads=[yab, wo_sb])
            ln_epilogue(fw, c, t, py, lnb[0], lnb[1])
            fw.dma("sp", x_out[nt * 128:(nt + 1) * 128, :], t[:], reads=[t])
        fw.finish()


GRID_W = 64
_prog_cache = {}

def prog(key, fn):
    if key not in _prog_cache:
        _prog_cache[key] = fn()
    return _prog_cache[key]


def ext_x(x, c, TPC):
    S = x.shape[0]
    out = np.zeros((TPC + 2 * HALO, x.shape[1]), x.dtype)
    lo, hi = c * TPC - HALO, (c + 1) * TPC + HALO
    a, b = max(lo, 0), min(hi, S)
    out[a - lo:b - lo] = x[a:b]
    return out


def rope_tables_T(S):
    t = np.arange(S)
    row, col = t // GRID_W, t % GRID_W
    half = 64
    freqs = (10000.0 ** (-np.arange(0, half, 2, dtype=np.float32) / half)).astype(np.float32)
    def ang(p):
        a = p.astype(np.float32)[:, None] * freqs[None, :]
        return np.concatenate([a, a], -1)
    ang_all = np.concatenate([ang(row), ang(col)], -1).astype(np.float32)
    return np.cos(ang_all).T.astype(np.float32), np.sin(ang_all).T.astype(np.float32)


def odd_small_params(inp, j):
    sp = np.zeros((128, 8 + 4 * 34), np.float32)
    sp[:, 0] = inp['q_norm_g'][j]
    sp[:, 1] = inp['k_norm_g'][j]
    for cc in range(4):
        base = 8 + cc * 34
        sl = slice(cc * 128, (cc + 1) * 128)
        sp[:, base:base + 31] = inp['conv_w'][j][:, sl].T
        sp[:, base + 31] = inp['conv_b'][j][sl]
        sp[:, base + 32] = inp['conv_ln_g'][j][sl]
        sp[:, base + 33] = inp['conv_ln_b'][j][sl]
    return sp


def run_odd_mixer(run, x, inp, w_in_bf, w_out_bf, l, j, TPC, NCO):
    S = TPC * NCO
    cosT, sinT = rope_tables_T(S)
    consts = host_consts2()
    sp = odd_small_params(inp, j)
    nc1 = prog(("o1", TPC), lambda: build_o1(TPC))
    maps = []
    for c in range(NCO):
        sl = slice(c * TPC, (c + 1) * TPC)
        maps.append(dict(x_ext=ext_x(x, c, TPC), w_in=w_in_bf, cs=np.stack([cosT[:, sl], sinT[:, sl]]), sp=sp, consts=consts))
    r1 = run(nc1, maps)
    kT = np.concatenate([r["kT"] for r in r1], axis=2)
    v = np.concatenate([r["v"] for r in r1], axis=0)
    nc2 = prog(("o2", TPC, S), lambda: build_o2(TPC, S, PK=min(2048, S)))
    lnp = np.stack([inp['ln_g'][l, 0], inp['ln_b'][l, 0]])
    maps = []
    for c in range(NCO):
        sl = slice(c * TPC, (c + 1) * TPC)
        maps.append(dict(x=x[sl], qT=r1[c]["qT"], yd=r1[c]["yd"], kT=kT, v=v, w_out=w_out_bf, lnp=lnp, consts=consts))
    r2 = run(nc2, maps)
    return np.concatenate([r["y"] for r in r2], axis=0)


POOL_WINDOWS = (2, 4, 8, 16)


def pool_inv_cnt(S):
    t = np.arange(S)
    out = np.zeros((4, S), np.float32)
    for gi, w in enumerate(POOL_WINDOWS):
        lo = np.clip(t - w // 2, 0, S - 1)
        hi = np.clip(t - w // 2 + w - 1, 0, S - 1)
        out[gi] = 1.0 / (hi - lo + 1).astype(np.float32)
    return out


def run_even_mixer(run, x, inp, w_in_bf, w_out_bf, pool_w_bf, l, j, TPC, NCO):
    S = TPC * NCO
    consts = host_consts3()
    lbl = np.ascontiguousarray(inp['hgrn_lb_logits'].reshape(2, 4, 4, 128).transpose(3, 0, 1, 2)).reshape(128, 32)
    nc1 = prog(("e1", TPC, l), lambda: build_e(TPC, l, True))
    maps = [dict(x_ext=ext_x(x, c, TPC), w_in=w_in_bf, lbl=lbl, consts=consts) for c in range(NCO)]
    r1 = run(nc1, maps)
    S_loc = [r["S_loc"] for r in r1]
    logD = [r["logD"] for r in r1]
    sp = np.zeros((128, 8), np.float32)
    sp[:, 0:4] = inp['hgrn_norm_g'][j].reshape(4, 128).T
    sp[:, 4:8] = inp['pool_scale'][j].reshape(4, 128).T
    ic = pool_inv_cnt(S)
    lnp = np.stack([inp['ln_g'][l, 0], inp['ln_b'][l, 0]])
    nc2 = prog(("e2", TPC, l), lambda: build_e(TPC, l, False))
    maps = []
    for c in range(NCO):
        S_d = np.zeros((8, 7, 128, 128), np.float32)
        L_d = np.zeros((128, 8, 7), np.float32)
        for dist in range(1, 8):
            cf, cb = c - dist, c + dist
            if cf >= 0:
                S_d[0:4, dist - 1] = S_loc[cf][0:4]
                L_d[:, 0:4, dist - 1] = logD[cf][:, 0:4]
            if cb < NCO:
                S_d[4:8, dist - 1] = S_loc[cb][4:8]
                L_d[:, 4:8, dist - 1] = logD[cb][:, 4:8]
        maps.append(dict(x_ext=ext_x(x, c, TPC), w_in=w_in_bf, lbl=lbl, consts=consts, w_out=w_out_bf, pool_w=pool_w_bf,
                         sp=sp, ic=np.ascontiguousarray(ic[:, c * TPC:(c + 1) * TPC]), S_d=S_d, L_d=L_d.reshape(128, 56), lnp=lnp))
    r2 = run(nc2, maps)
    return np.concatenate([r["y"] for r in r2], axis=0)


NCORES = 8
TPC_FULL = 2048


def _run(nc, in_maps):
    res = run_bass_kernel_spmd(nc, in_maps, core_ids=list(range(len(in_maps))))
    return res.results


def _convert_weights(ws):
    flat = np.concatenate([np.ascontiguousarray(w).reshape(-1) for w in ws])
    n = flat.size
    per = 128 * NCORES
    NEL = -(-n // per)
    pad = np.zeros(NEL * per, np.float32)
    pad[:n] = flat
    sh = pad.reshape(NCORES, 128, NEL)
    nc = prog(("cvt", NEL), lambda: build_convert(NEL))
    res = _run(nc, [{"w": sh[c]} for c in range(NCORES)])
    ob = np.concatenate([r["o"].reshape(-1) for r in res])[:n]
    outs = []
    off = 0
    for w in ws:
        outs.append(ob[off:off + w.size].reshape(w.shape))
        off += w.size
    return outs


def kernel(**inputs):
    inp = {k: np.asarray(v) for k, v in inputs.items()}
    TPC, NCO = TPC_FULL, NCORES
    names = ['w_in_ab', 'w_out_ab', 'pool_w', 'w_in_cd', 'w_out_cd', 'xa_wq', 'xa_wkv', 'xa_wo', 'ffn_w_gu', 'ffn_w_down']
    wb = dict(zip(names, _convert_weights([inp[k] for k in names])))
    x = np.ascontiguousarray(inp['x'][0])
    mem = np.ascontiguousarray(inp['mem'][0])
    consts_t = host_consts()
    nct = prog(("tail", TPC), lambda: build_tail(TPC))
    for l in range(4):
        j = l // 2
        if l % 2 == 0:
            x = run_even_mixer(_run, x, inp, wb['w_in_ab'][j], wb['w_out_ab'][j], wb['pool_w'][j], l, j, TPC, NCO)
        else:
            x = run_odd_mixer(_run, x, inp, wb['w_in_cd'][j], wb['w_out_cd'][j], l, j, TPC, NCO)
        lnp = np.stack([inp['ln_g'][l, 1], inp['ln_b'][l, 1], inp['ln_g'][l, 2], inp['ln_b'][l, 2]])
        maps = [dict(x=x[c * TPC:(c + 1) * TPC], mem=mem, wq=wb['xa_wq'][l], wkv=wb['xa_wkv'][l], wo=wb['xa_wo'][l],
                     wgu=wb['ffn_w_gu'][l], wdn=wb['ffn_w_down'][l], lnp=lnp, consts=consts_t) for c in range(NCO)]
        r = _run(nct, maps)
        x = np.concatenate([q["y"] for q in r], axis=0)
    return x[None].astype(np.float32)
```
